# Optimizing a Trainium2 kernel written in Bass

```python
import jax, jax.numpy as jnp
from jax import lax
import numpy as np

D_MODEL = 1024
BATCH = 8
SEQ = 2048
DEPTH = 1

D_MIX = D_MODEL
D_ATTN = D_MIX // 2
D_POOL = D_MIX - D_ATTN
HEAD_DIM = 64
N_HEADS = D_ATTN // HEAD_DIM
POOL_WINDOWS = (2, 4, 8, 16)
N_POOL_GROUPS = len(POOL_WINDOWS)
POOL_GROUP_DIM = D_POOL // N_POOL_GROUPS
Q_BLOCK = 128
EPS = 1e-6
D_IN = 4 * D_ATTN + 2 * D_POOL

kernel_name = "stickbreak_pool_hybrid_block"


def rms_norm(x, g):
    xf = x.astype(jnp.float32)
    y = xf * lax.rsqrt(jnp.mean(xf * xf, axis=-1, keepdims=True) + EPS)
    return (y * g.astype(jnp.float32)).astype(x.dtype)


def stick_breaking_attention(q, k, v):
    B, H, S, Dh = q.shape
    nb = S // Q_BLOCK
    qf = q.astype(jnp.float32) * (Dh ** -0.5)
    kb = k.astype(jnp.float32).reshape(B, H, nb, Q_BLOCK, Dh).transpose(2, 0, 1, 3, 4)
    vb = v.astype(jnp.float32).reshape(B, H, nb, Q_BLOCK, Dh).transpose(2, 0, 1, 3, 4)
    offs = jnp.arange(Q_BLOCK, dtype=jnp.int32)
    outs = []
    for i in range(nb):
        q_i = qf[:, :, i * Q_BLOCK:(i + 1) * Q_BLOCK]
        q_pos = i * Q_BLOCK + offs

        def step(carry, xs, q_i=q_i, q_pos=q_pos):
            acc, log_rem = carry
            k_j, v_j, j = xs
            k_pos = j * Q_BLOCK + offs
            mask = k_pos[None, :] < q_pos[:, None]
            z = jnp.einsum('bhqd,bhkd->bhqk', q_i, k_j)
            log_one_minus = jnp.where(mask, jax.nn.log_sigmoid(-z), 0.0)
            later = lax.cumsum(log_one_minus, axis=3, reverse=True) - log_one_minus
            log_w = jax.nn.log_sigmoid(z) + later + log_rem
            w = jnp.where(mask, jnp.exp(log_w), 0.0)
            acc = acc + jnp.einsum('bhqk,bhkd->bhqd', w, v_j)
            log_rem = log_rem + jnp.sum(log_one_minus, axis=3, keepdims=True)
            return (acc, log_rem), None

        idx = jnp.arange(i, -1, -1, dtype=jnp.int32)
        init = (jnp.zeros((B, H, Q_BLOCK, Dh), jnp.float32),
                jnp.zeros((B, H, Q_BLOCK, 1), jnp.float32))
        (acc, _), _ = lax.scan(step, init, (kb[idx], vb[idx], idx))
        outs.append(acc)
    return jnp.concatenate(outs, axis=2).astype(v.dtype)


def multiscale_causal_pool(u):
    B, S, _ = u.shape
    uf = u.astype(jnp.float32).reshape(B, S, N_POOL_GROUPS, POOL_GROUP_DIM)
    cs = jnp.concatenate([jnp.zeros((B, 1, N_POOL_GROUPS, POOL_GROUP_DIM), jnp.float32),
                          jnp.cumsum(uf, axis=1)], axis=1)
    t = jnp.arange(S, dtype=jnp.int32)[:, None]
    win = jnp.array(POOL_WINDOWS, dtype=jnp.int32)[None, :]
    lo = jnp.maximum(t + 1 - win, 0)
    grp = jnp.arange(N_POOL_GROUPS, dtype=jnp.int32)[None, :]
    window_sum = cs[:, 1:] - cs[:, lo, grp]
    count = (t + 1 - lo).astype(jnp.float32)
    pooled = window_sum / count[None, :, :, None] - uf
    return pooled.astype(u.dtype)


def setup_inputs(seed: int = 0) -> dict:
    key = jax.random.key(seed)
    ks = jax.random.split(key, 14)
    L = DEPTH
    x = jax.random.normal(ks[0], (BATCH, SEQ, D_MODEL), jnp.float32)
    c = jax.random.normal(ks[1], (BATCH, D_MODEL), jnp.float32)
    w_ada = jax.random.normal(ks[2], (L, D_MODEL, 3 * D_MODEL), jnp.float32) * (0.5 * D_MODEL ** -0.5)
    b_ada = jax.random.normal(ks[3], (L, 3 * D_MODEL), jnp.float32) * 0.02
    norm_g = 1.0 + 0.02 * jax.random.normal(ks[4], (L, D_MODEL), jnp.float32)
    w_in = jax.random.normal(ks[5], (L, D_MODEL, D_IN), jnp.float32) * (D_MODEL ** -0.5)
    q_norm_g = 1.0 + 0.02 * jax.random.normal(ks[6], (L, HEAD_DIM), jnp.float32)
    k_norm_g = 1.0 + 0.02 * jax.random.normal(ks[7], (L, HEAD_DIM), jnp.float32)
    w_pool = jax.random.normal(ks[8], (L, N_POOL_GROUPS, POOL_GROUP_DIM, POOL_GROUP_DIM), jnp.float32) * (POOL_GROUP_DIM ** -0.5)
    b_pool = jax.random.normal(ks[9], (L, N_POOL_GROUPS, POOL_GROUP_DIM), jnp.float32) * 0.02
    pool_scale = 1.0 + 0.1 * jax.random.normal(ks[10], (L, D_POOL), jnp.float32)
    w_out = jax.random.normal(ks[11], (L, D_MIX, D_MODEL), jnp.float32) * (D_MIX ** -0.5)
    return {"x": x, "c": c, "w_ada": w_ada, "b_ada": b_ada, "norm_g": norm_g,
            "w_in": w_in, "q_norm_g": q_norm_g, "k_norm_g": k_norm_g,
            "w_pool": w_pool, "b_pool": b_pool, "pool_scale": pool_scale, "w_out": w_out}


def reference(x, c, w_ada, b_ada, norm_g, w_in, q_norm_g, k_norm_g,
              w_pool, b_pool, pool_scale, w_out):
    B, S, D = x.shape
    h = x
    c_act = jax.nn.silu(c)
    for l in range(DEPTH):
        mod = c_act @ w_ada[l] + b_ada[l]
        shift, scale, gate = jnp.split(mod, 3, axis=-1)
        hn = rms_norm(h, norm_g[l]) * (1.0 + scale[:, None, :]) + shift[:, None, :]

        proj = hn @ w_in[l]
        q, k, v, g_attn, u, g_pool = jnp.split(
            proj, [D_ATTN, 2 * D_ATTN, 3 * D_ATTN, 4 * D_ATTN, 4 * D_ATTN + D_POOL], axis=-1)

        q = rms_norm(q.reshape(B, S, N_HEADS, HEAD_DIM), q_norm_g[l]).transpose(0, 2, 1, 3)
        k = rms_norm(k.reshape(B, S, N_HEADS, HEAD_DIM), k_norm_g[l]).transpose(0, 2, 1, 3)
        v = v.reshape(B, S, N_HEADS, HEAD_DIM).transpose(0, 2, 1, 3)
        attn = stick_breaking_attention(q, k, v).transpose(0, 2, 1, 3).reshape(B, S, D_ATTN)
        attn = attn * jax.nn.silu(g_attn)

        pooled = multiscale_causal_pool(u)
        pooled = jnp.einsum('bsgc,gcd->bsgd', pooled, w_pool[l]) + b_pool[l]
        pool_out = pooled.reshape(B, S, D_POOL) * pool_scale[l]
        pool_out = pool_out * jax.nn.silu(g_pool)

        y = jnp.concatenate([attn, pool_out], axis=-1) @ w_out[l]
        h = h + gate[:, None, :] * y
    return h
```

```python
import os
import numpy as np
from contextlib import ExitStack
import concourse.bass as bass
import concourse.mybir as mybir
from concourse.bass_utils import run_bass_kernel_spmd

F32 = mybir.dt.float32
BF16 = mybir.dt.bfloat16
AF = mybir.ActivationFunctionType
ALU = mybir.AluOpType

S = 2048
D = 1024
NT = 16
KC = 8
EPS = 1e-6
NDS = 56


class Tok:
    __slots__ = ("name", "w", "r")

    def __init__(self, name):
        self.name = name
        self.w = {}
        self.r = {}


def toks(prefix, n):
    return [Tok(f"{prefix}{i}") for i in range(n)]


def alias(new, old):
    w, r = {}, {}
    for t in old:
        for k, v in t.w.items():
            w[k] = max(w.get(k, 0), v)
        for k, v in t.r.items():
            r[k] = max(r.get(k, 0), v)
    for t in new:
        t.w = dict(w)
        t.r = dict(r)


class Prog:
    def __init__(self, nc, es):
        self.nc = nc
        self.eng = {"pe": nc.tensor, "act": nc.scalar, "dve": nc.vector,
                    "pool": nc.gpsimd, "sp": nc.sync}
        self.sem = {k: es.enter_context(nc.semaphore("s_" + k))
                    for k in ("pe", "act", "dve", "pool")}
        self.cnt = {k: 0 for k in self.sem}
        self.seen = {k: {} for k in self.eng}
        self.dsems = [es.enter_context(nc.semaphore(f"dq{i}")) for i in range(NDS)]
        self.dval = [0] * NDS
        self.dpool = {"pool": list(range(0, 24)), "sp": list(range(24, NDS)), "act": list(range(24, NDS))}
        self.dnext = {"pool": 0, "sp": 0, "act": 0}
        self.nwaits = 0

    def _semh(self, key):
        return self.sem[key] if isinstance(key, str) else self.dsems[key[1]]

    def _wait(self, eng, deps):
        for key, val in deps.items():
            if self.seen[eng].get(key, 0) >= val:
                continue
            self.eng[eng].wait_ge(self._semh(key), val)
            self.seen[eng][key] = val
            self.nwaits += 1

    def _deps(self, eng, reads, writes, is_dma=False):
        deps = {}

        def need(evs, kind):
            for key, val in evs.items():
                if not is_dma and key == eng and eng == "pe":
                    continue
                deps[key] = max(deps.get(key, 0), val)

        for t in reads:
            need(t.w, "raw")
        for t in writes:
            need(t.w, "waw")
            need(t.r, "war")
        return deps

    def op(self, eng, fn, reads=(), writes=()):
        deps = self._deps(eng, reads, writes)
        self._wait(eng, deps)
        ins = fn(self.eng[eng])
        self.cnt[eng] += 1
        ins.then_inc(self.sem[eng], 1)
        c = self.cnt[eng]
        for t in reads:
            t.r[eng] = c
        for t in writes:
            t.w = {eng: c}
            t.r = {}

    def dma(self, q, out, in_, reads=(), writes=(), **kw):
        deps = self._deps(q, reads, writes, is_dma=True)
        pool_ = self.dpool[q]
        i = pool_[self.dnext[q] % len(pool_)]
        self.dnext[q] += 1
        if self.dval[i] > 0:
            key = ("d", i)
            deps[key] = max(deps.get(key, 0), self.dval[i])
        self._wait(q, deps)
        self.dval[i] += 16
        self.eng[q].dma_start(out=out, in_=in_, **kw).then_inc(self.dsems[i], 16)
        key = ("d", i)
        for t in reads:
            t.r[key] = self.dval[i]
        for t in writes:
            t.w = {key: self.dval[i]}
            t.r = {}

    def finish(self, q="sp"):
        deps = {("d", i): v for i, v in enumerate(self.dval) if v > 0}
        for k in self.sem:
            if self.cnt[k] > 0:
                deps[k] = self.cnt[k]
        self._wait(q, deps)


def build_nc(debug=False):
    nc = bass.Bass("TRN2", target_bir_lowering=False)
    es = ExitStack()

    def dram_in(name, shape):
        return nc.dram_tensor(name, list(shape), F32, kind="ExternalInput").ap()

    x_d = dram_in("x", [S, D])
    c_d = dram_in("c_pk", [128, 8])
    wada_d = dram_in("w_ada", [D, 3 * D])
    bada_d = dram_in("b_ada", [1, 3 * D])
    ng_d = dram_in("norm_g", [1, D])
    win_d = dram_in("w_in", [D, 3 * D])
    qg_d = dram_in("qg", [128, 1])
    kg_d = dram_in("kg", [128, 1])
    wpool_d = dram_in("w_pool", [128, 4, 128])
    bp_d = dram_in("bpT", [128, 4])
    psT_d = dram_in("psT", [128, 4])
    psbc_d = dram_in("ps_bc", [128, 4, 128])
    wout_d = dram_in("w_out", [D, D])
    y_d = nc.dram_tensor("y", [S, D], F32, kind="ExternalOutput").ap()
    dbg = {}

    def dbg_out(name, shape):
        dbg[name] = nc.dram_tensor(name, list(shape), F32, kind="ExternalOutput").ap()
        return dbg[name]

    P = Prog(nc, es)

    def sb(name, shape, dt=F32):
        return es.enter_context(nc.sbuf_tensor(name, list(shape), dt))

    RA = sb("RA", [128, 24576], BF16)
    RB = sb("RB", [128, 12288], BF16)
    RC = sb("RC", [128, 16384], BF16)
    RG = sb("RG", [128, 8192], BF16)
    RQ = sb("RQ", [128, 8192], BF16)
    RP = sb("RP", [128, 17408], BF16)
    RT = sb("RT", [128, 5120], BF16)
    gate_bc = sb("gate_bc", [128, 1024], F32)

    def v16(reg, off, shape):
        n = int(np.prod(shape[1:]))
        ap = reg[:, off // 2: off // 2 + n]
        if len(shape) == 3:
            ap = ap.rearrange("p (a b) -> p a b", a=shape[1])
        return ap

    def v32(reg, off, shape):
        n = int(np.prod(shape[1:]))
        r32 = reg.bitcast(F32)
        ap = r32[:, off // 4: off // 4 + n]
        if len(shape) == 3:
            ap = ap.rearrange("p (a b) -> p a b", a=shape[1])
        return ap

    KB = 1024
    wada = v16(RA, 0, [128, 8, 3072])
    qT = v16(RA, 0, [128, 4, S])
    kT = v16(RA, 16 * KB, [128, 4, S])
    v_sb = v16(RA, 32 * KB, [128, NT, 512])
    win_ring = [v16(RB, i * 8 * KB, [128, 8, 512]) for i in range(3)]
    hnT = v16(RC, 0, [128, 8, S])
    attnT = v16(RC, 0, [128, 4, S])
    wout_bf = v16(RC, 16 * KB, [128, 8, D])
    shift_bc = v32(RG, 0, [128, D])
    gp_bc = v32(RG, 4 * KB, [128, D])
    sgaT = v16(RG, 0, [128, 4, S])
    poolT = v16(RQ, 0, [128, 4, S])

    PS = [es.enter_context(nc.psum_tensor(f"PS{i}", [128, 1024], F32)) for i in range(4)]
    PZ = [PS[2], PS[3]]

    def bank(i):
        return PS[i // 2][:, (i % 2) * 512:(i % 2 + 1) * 512]

    bk = toks("bank", 8)

    iota_f = sb("iota_f", [128, 128], F32)
    ident_bf = sb("ident_bf", [128, 128], BF16)
    negT_bf = sb("negT_bf", [128, 128], BF16)
    negO_bf = sb("negO_bf", [128, 128], BF16)
    bd_bf = sb("bd_bf", [128, 128], BF16)
    maskb_bf = sb("maskb_bf", [128, 128], BF16)
    ones_f = sb("ones_f", [1, 128], F32)
    small = sb("small", [128, 64], F32)
    cpk = small[:, 0:8]
    ce = small[:, 8:16]
    ssq = small[:, 16:32]
    rstd = small[:, 32:48]
    qg = small[:, 48:49]
    kg = small[:, 49:50]
    bpT = small[:, 50:54]
    psT = small[:, 54:58]
    bps = small[:, 58:62]
    cact_bf = sb("cact_bf", [128, 8], BF16)
    wpool_f = sb("wpool_f", [128, 4, 128], F32)
    psbc = sb("psbc", [128, 4, 128], F32)
    wpool_bf = sb("wpool_bf", [128, 4, 128], BF16)

    t_iota, t_const, t_small, t_cact = Tok("iota"), Tok("const"), Tok("small"), Tok("cact")
    t_wpf, t_psbc, t_wpb = Tok("wpf"), Tok("psbc"), Tok("wpb")

    P.op("pool", lambda e: e.iota(iota_f[:], pattern=[[1, 128]], base=0, channel_multiplier=-1,
                                  allow_small_or_imprecise_dtypes=True), writes=[t_iota])
    P.op("pool", lambda e: e.memset(negO_bf[:], -1.0), writes=[t_const])
    P.op("pool", lambda e: e.memset(bd_bf[:], 0.0), writes=[t_const])
    P.op("pool", lambda e: e.memset(bd_bf[0:64, 0:64], 1.0), writes=[t_const])
    P.op("pool", lambda e: e.memset(bd_bf[64:128, 64:128], 1.0), writes=[t_const])
    P.op("pool", lambda e: e.memset(ones_f[:], 1.0), writes=[t_const])
    zeros_bf = sb("zeros_bf", [128, 512], BF16)
    P.op("pool", lambda e: e.memset(zeros_bf[:], 0.0), writes=[t_const])
    P.op("pool", lambda e: e.memset(small[:, 16:48], 0.0), writes=[t_small])
    P.op("dve", lambda e: e.tensor_single_scalar(out=ident_bf[:], in_=iota_f[:], scalar=0.0,
                                                 op=ALU.is_equal), reads=[t_iota], writes=[t_const])
    P.op("dve", lambda e: e.tensor_scalar(out=negT_bf[:], in0=iota_f[:], scalar1=0.0, scalar2=-1.0,
                                          op0=ALU.is_le, op1=ALU.mult), reads=[t_iota], writes=[t_const])
    P.op("dve", lambda e: e.tensor_scalar(out=maskb_bf[:], in0=iota_f[:], scalar1=0.0,
                                          scalar2=-30000.0, op0=ALU.is_le, op1=ALU.mult),
         reads=[t_iota], writes=[t_const])

    RGf = RG.bitcast(F32)
    bada_row = RGf[0:1, 0:3072]
    mod_row = bada_row
    g_row = RGf[0:1, 3072:4096]
    gp_row = RT.bitcast(F32)[0:1, 0:1024]
    t_rows = Tok("rows")
    P.dma("sp", cpk, c_d, writes=[t_small])
    P.dma("sp", bada_row, bada_d, writes=[t_rows])
    P.dma("sp", g_row, ng_d, writes=[t_rows])
    P.dma("sp", qg, qg_d, writes=[t_small])
    P.dma("sp", kg, kg_d, writes=[t_small])
    P.dma("sp", bpT, bp_d, writes=[t_small])
    P.dma("sp", psT, psT_d, writes=[t_small])
    P.dma("sp", wpool_f[:], wpool_d, writes=[t_wpf])
    P.dma("sp", psbc[:], psbc_d, writes=[t_psbc])

    t_wada = toks("wada", 8)
    t_pace = toks("pace", NT)

    P.op("act", lambda e: e.activation(out=ce, in_=cpk, func=AF.Exp, scale=-1.0),
         reads=[t_small], writes=[t_small])
    P.op("dve", lambda e: e.tensor_scalar(out=ce, in0=ce, scalar1=1.0, scalar2=None, op0=ALU.add),
         reads=[t_small], writes=[t_small])
    P.op("dve", lambda e: e.reciprocal(out=ce, in_=ce), reads=[t_small], writes=[t_small])
    P.op("dve", lambda e: e.tensor_tensor(out=cact_bf[:], in0=cpk, in1=ce, op=ALU.mult),
         reads=[t_small], writes=[t_cact])
    P.op("dve", lambda e: e.tensor_scalar(out=qg, in0=qg, scalar1=0.125, scalar2=None, op0=ALU.mult),
         reads=[t_small], writes=[t_small])
    P.op("dve", lambda e: e.tensor_tensor(out=bps, in0=bpT, in1=psT, op=ALU.mult),
         reads=[t_small], writes=[t_small])
    P.op("dve", lambda e: e.tensor_tensor(out=wpool_bf[:], in0=wpool_f[:], in1=psbc[:], op=ALU.mult),
         reads=[t_wpf, t_psbc], writes=[t_wpb])

    xs = [v32(RP, i * 4 * KB, [128, D]) for i in range(3)] + [v32(RP, 30 * KB, [128, D])]
    xn_buf = v16(RP, 12 * KB, [128, 8, D])
    junk = v16(RP, 28 * KB, [128, D])
    t_xs, t_xn, t_junk = toks("xs", 4), toks("xn", 8), Tok("junk")
    t_hnT = toks("hnT", NT)
    lnv1 = small[:, 62:63]

    def psT_view(gq, kc):
        b = 4 * (gq % 2) + kc // 2
        return PS[b // 2].bitcast(BF16)[:, (b % 2) * 1024 + (kc % 2) * 512:(b % 2) * 1024 + (kc % 2) * 512 + 512], b

    for t in range(NT):
        b3, b8 = t % 4, t % 8
        gq, tl = t // 4, t % 4
        P.dma("sp", xs[b3], x_d[t * 128:(t + 1) * 128, :], writes=[t_xs[b3]])
        P.op("dve", lambda e, t=t, b3=b3: e.scalar_tensor_tensor(
            out=junk, in0=xs[b3], scalar=1.0, in1=xs[b3], op0=ALU.mult, op1=ALU.mult,
            accum_out=ssq[:, t:t + 1]), reads=[t_xs[b3], t_small], writes=[t_junk, t_small])
        P.op("act", lambda e, t=t: e.activation(out=lnv1, in_=ssq[:, t:t + 1], func=AF.Ln,
                                                scale=1.0 / D, bias=EPS),
             reads=[t_small], writes=[t_small])
        P.op("act", lambda e, t=t: e.activation(out=rstd[:, t:t + 1], in_=lnv1, func=AF.Exp, scale=-0.5),
             reads=[t_small], writes=[t_small])
        P.op("act", lambda e, t=t, b3=b3, b8=b8: e.activation(out=xn_buf[:, b8, :], in_=xs[b3], func=AF.Copy,
                                                              scale=rstd[:, t:t + 1]),
             reads=[t_xs[b3], t_small], writes=[t_xn[b8], t_pace[t]])
        if t % 2 == 1:
            kcw = t // 2
            P.dma("pool", wada[:, kcw, :], wada_d[kcw * 128:(kcw + 1) * 128, :], reads=[t_pace[t]],
                  writes=[t_wada[kcw]])

        def ftr(e, b8=b8, gq=gq, tl=tl):
            ins = None
            for kc in range(KC):
                pv, _ = psT_view(gq, kc)
                ins = e.transpose(pv[:, tl * 128:(tl + 1) * 128], xn_buf[:, b8, kc * 128:(kc + 1) * 128],
                                  ident_bf[:])
            return ins
        bks = [bk[4 * (gq % 2) + i] for i in range(4)]
        P.op("pe", ftr, reads=[t_xn[b8], t_const], writes=bks)
        if (tl == 0 and gq >= 1) or t == NT - 1:
          for gq in ([gq - 1] if t < NT - 1 else ([gq - 1, gq] if tl == 0 else [gq])):
              for kc in range(KC):
                  pv, b = psT_view(gq, kc)
                  dst = hnT[:, kc, gq * 512:(gq + 1) * 512]
                  if kc % 2 == 0:
                      P.op("act", lambda e, pv=pv, dst=dst: e.activation(out=dst, in_=pv, func=AF.Copy),
                           reads=[bk[b]], writes=t_hnT[gq * 4:(gq + 1) * 4])
                  else:
                      P.op("dve", lambda e, pv=pv, dst=dst: e.tensor_copy(out=dst, in_=pv),
                           reads=[bk[b]], writes=t_hnT[gq * 4:(gq + 1) * 4])


    for kc in range(KC):
        def f(e, kc=kc):
            ins = None
            for n in range(6):
                ins = e.matmul(bank(n)[0:1, :], cact_bf[:, kc:kc + 1], wada[:, kc, n * 512:(n + 1) * 512],
                               start=(kc == 0), stop=(kc == KC - 1))
            return ins
        P.op("pe", f, reads=[t_cact, t_wada[kc]], writes=bk[0:6])
    for n in range(6):
        P.op("dve", lambda e, n=n: e.tensor_tensor(out=mod_row[:, n * 512:(n + 1) * 512],
                                                   in0=bank(n)[0:1, :],
                                                   in1=bada_row[:, n * 512:(n + 1) * 512], op=ALU.add),
             reads=[bk[n], t_rows], writes=[t_rows])
    P.op("dve", lambda e: e.scalar_tensor_tensor(out=gp_row, in0=mod_row[:, 1024:2048], scalar=1.0,
                                                 in1=g_row, op0=ALU.add, op1=ALU.mult),
         reads=[t_rows], writes=[t_rows])
    t_bc = Tok("bc")
    modT = sb("modT", [128, 16], F32)
    t_modT = Tok("modT")
    for n in range(2):
        P.op("pe", lambda e, n=n: e.matmul(bank(n), ones_f[0:1, :], mod_row[:, 2048 + n * 512:2560 + n * 512],
                                           start=True, stop=True),
             reads=[t_rows, t_const], writes=[bk[n]])
        P.op("act", lambda e, n=n: e.activation(out=gate_bc[:, n * 512:(n + 1) * 512], in_=bank(n), func=AF.Copy),
             reads=[bk[n]], writes=[t_bc])

    def fmodT(e):
        ins = None
        for i in range(8):
            ins = e.matmul(bank(2)[:, i:i + 1], mod_row[:, i * 128:(i + 1) * 128], ones_f[0:1, 0:1],
                           start=True, stop=True)
        for i in range(8):
            ins = e.matmul(bank(2)[:, 8 + i:9 + i], gp_row[:, i * 128:(i + 1) * 128], ones_f[0:1, 0:1],
                           start=True, stop=True)
        return ins
    P.op("pe", fmodT, reads=[t_rows, t_const], writes=[bk[2]])
    P.op("dve", lambda e: e.tensor_copy(out=modT[:], in_=bank(2)[:, 0:16]), reads=[bk[2]], writes=[t_modT])

    for kc in range(KC):
        for hf in range(2):
            view = hnT[:, kc, hf * 1024:(hf + 1) * 1024]
            tk = t_hnT[hf * 8:(hf + 1) * 8]
            if (kc + hf) % 2 == 0:
                P.op("act", lambda e, view=view, kc=kc: e.activation(
                    out=view, in_=view, func=AF.Identity, scale=modT[:, 8 + kc:9 + kc], bias=modT[:, kc:kc + 1]),
                    reads=tk + [t_modT], writes=tk)
            else:
                P.op("dve", lambda e, view=view, kc=kc: e.tensor_scalar(
                    out=view, in0=view, scalar1=modT[:, 8 + kc:9 + kc], scalar2=modT[:, kc:kc + 1],
                    op0=ALU.mult, op1=ALU.add), reads=tk + [t_modT], writes=tk)

    if debug:
        d_hnT = nc.dram_tensor("d_hnT", [128, 8, S], F32, kind="ExternalOutput").ap()
        for kc in range(KC):
            P.dma("pool", d_hnT[:, kc, :], hnT[:, kc, :], reads=t_hnT)
        d_bc = nc.dram_tensor("d_bc", [128, 3, D], F32, kind="ExternalOutput").ap()
        P.dma("sp", d_bc[:, 2, :], gate_bc[:], reads=[t_bc])
        d_small = nc.dram_tensor("d_small", [128, 64], F32, kind="ExternalOutput").ap()
        P.dma("sp", d_small, small[:], reads=[t_small])

    t_qT = [toks(f"qT{c}_", 4) for c in range(4)]
    t_kT = [toks(f"kT{c}_", 4) for c in range(4)]
    t_v = toks("v", NT)
    alias([t for l in t_qT for t in l] + [t for l in t_kT for t in l] + t_v, t_wada)
    t_sga = [toks(f"sga{c}_", 4) for c in range(4)]
    alias([t for l in t_sga for t in l], [t_rows])
    t_bc2 = Tok("gate_bc")
    t_bc2.w = dict(t_bc.w)
    t_poolT = [toks(f"poolT{g}_", 4) for g in range(4)]

    UW = 16 + S
    u_pad = v32(RP, 0, [128, UW])
    sA = v32(RP, 8256, [128, UW])
    sB = v32(RP, 16512, [128, UW])
    pooled2 = [v16(RP, 24768, [128, S]), v16(RP, 24768 + 4096, [128, S])]
    invc = small[:, 0:16]
    tmp16 = sb("tmp16", [128, 16], F32)
    iota16 = sb("iota16", [128, 16], F32)
    t_u, t_sA, t_sB, t_pooled = Tok("u"), Tok("sA"), Tok("sB"), toks("pooled", 2)
    alias([t_u, t_sA, t_sB] + t_pooled, t_xs + t_xn + [t_junk])
    t_iota16 = Tok("iota16")
    P.op("pool", lambda e: e.iota(iota16[:], pattern=[[1, 16]], base=1, channel_multiplier=0,
                                  allow_small_or_imprecise_dtypes=True), writes=[t_iota16])

    sq_sb = [v16(RT, i * KB, [128, 512]) for i in range(2)]
    ln_sb = [v32(RT, 2 * KB + i * 2 * KB, [128, 512]) for i in range(2)]
    se_sb = [v32(RT, 6 * KB + i * 2 * KB, [128, 512]) for i in range(2)]
    t_sq, t_ln, t_se = toks("sq", 2), toks("ln", 2), toks("se", 2)
    alias(t_sq + t_ln + t_se, [t_rows])

    t_win = toks("win", 3)
    CG_ORDER = [4, 5, 0, 1, 2, 3]

    def load_cg(idx, after=()):
        cg = CG_ORDER[idx]
        slot = idx % 3
        src = win_d.rearrange("(kc p) n -> p kc n", p=128)
        for half in range(2):
            P.dma("pool", win_ring[slot][:, half * 4:(half + 1) * 4, :],
                  src[:, half * 4:(half + 1) * 4, cg * 512:(cg + 1) * 512], reads=list(after),
                  writes=[t_win[slot]])

    load_cg(0)
    load_cg(1, after=[t_xs[(NT - 1) % 4]])
    load_cg(2, after=[t_xs[(NT - 1) % 4]])

    cstate = {"psc": 0, "psn": 0, "u": 0}

    def next_psc():
        i = cstate["psc"]
        cstate["psc"] = (i + 1) % 4
        return i

    def inproj_fm(slot, lc, tg):
        bi = next_psc()

        def f(e):
            ins = None
            for kc in range(KC):
                ins = e.matmul(bank(bi), win_ring[slot][:, kc, lc * 128:(lc + 1) * 128],
                               hnT[:, kc, tg * 512:(tg + 1) * 512], start=(kc == 0), stop=(kc == KC - 1))
            return ins
        P.op("pe", f, reads=[t_win[slot]] + t_hnT[tg * 4:(tg + 1) * 4], writes=[bk[bi]])
        return bi

    def silu_evac(bi, dst, t_dst):
        u = cstate["u"]
        cstate["u"] = 1 - u
        P.op("act", lambda e: e.activation(out=se_sb[u], in_=bank(bi), func=AF.Exp, scale=-1.0),
             reads=[bk[bi]], writes=[t_se[u]])
        P.op("act", lambda e: e.activation(out=se_sb[u], in_=se_sb[u], func=AF.Ln, bias=1.0),
             reads=[t_se[u]], writes=[t_se[u]])
        P.op("act", lambda e: e.activation(out=se_sb[u], in_=se_sb[u], func=AF.Exp, scale=-1.0),
             reads=[t_se[u]], writes=[t_se[u]])
        P.op("dve", lambda e: e.tensor_tensor(out=dst, in0=bank(bi), in1=se_sb[u], op=ALU.mult),
             reads=[bk[bi], t_se[u]], writes=[t_dst])

    def qk_stage1(bi):
        u = cstate["psn"]
        cstate["psn"] = 1 - u
        P.op("act", lambda e: e.activation(out=sq_sb[u], in_=bank(bi), func=AF.Square),
             reads=[bk[bi]], writes=[t_sq[u]])
        return u

    def qk_stage2(bi, u, dst, t_dst, gcol):
        bn = 4 + u
        P.op("pe", lambda e: e.matmul(bank(bn), bd_bf[:], sq_sb[u], start=True, stop=True),
             reads=[t_sq[u], t_const], writes=[bk[bn]])
        P.op("act", lambda e: e.activation(out=ln_sb[u], in_=bank(bn), func=AF.Ln, scale=1.0 / 64.0, bias=EPS),
             reads=[bk[bn]], writes=[t_ln[u]])
        P.op("act", lambda e: e.activation(out=ln_sb[u], in_=ln_sb[u], func=AF.Exp, scale=-0.5),
             reads=[t_ln[u]], writes=[t_ln[u]])
        P.op("dve", lambda e: e.scalar_tensor_tensor(out=dst, in0=bank(bi), scalar=gcol, in1=ln_sb[u],
                                                     op0=ALU.mult, op1=ALU.mult),
             reads=[bk[bi], t_ln[u], t_small], writes=[t_dst])

    def pool_pre(g):
        win = 2 << g
        pooled, tp = pooled2[g % 2], t_pooled[g % 2]
        sbuf = [sA, sB]
        tk = [t_sA, t_sB]
        src, tsrc = u_pad, t_u
        sh = 1
        k = 0
        while sh < win:
            dst, tdst = sbuf[k % 2], tk[k % 2]
            P.op("dve", lambda e, src=src, dst=dst, sh=sh: e.tensor_tensor(
                out=dst[:, 16:UW], in0=src[:, 16:UW], in1=src[:, 16 - sh:UW - sh], op=ALU.add),
                reads=[tsrc], writes=[tdst])
            src, tsrc = dst, tdst
            sh *= 2
            k += 1
        P.op("dve", lambda e, src=src: e.scalar_tensor_tensor(
            out=pooled, in0=src[:, 16:UW], scalar=1.0 / win, in1=u_pad[:, 16:UW],
            op0=ALU.mult, op1=ALU.subtract), reads=[tsrc, t_u], writes=[tp])
        P.op("dve", lambda e: e.tensor_scalar(out=invc, in0=iota16[:], scalar1=float(win), scalar2=None,
                                              op0=ALU.min), reads=[t_iota16, t_small], writes=[t_small])
        P.op("dve", lambda e: e.reciprocal(out=invc, in_=invc), reads=[t_small], writes=[t_small])
        P.op("dve", lambda e, src=src: e.tensor_tensor(out=tmp16[:], in0=src[:, 16:32], in1=invc, op=ALU.mult),
             reads=[tsrc, t_small], writes=[t_small])
        P.op("dve", lambda e: e.tensor_tensor(out=pooled[:, 0:16], in0=tmp16[:], in1=u_pad[:, 16:32],
                                              op=ALU.subtract), reads=[t_small, t_u, tp], writes=[tp])

    def pool_mm(g):
        pooled, tp = pooled2[g % 2], t_pooled[g % 2]
        for tg in range(4):
            bi = 6 + (tg % 2)
            P.op("pe", lambda e, tg=tg, bi=bi: e.matmul(bank(bi), wpool_bf[:, g, :],
                                                        pooled[:, tg * 512:(tg + 1) * 512],
                                                        start=True, stop=True),
                 reads=[t_wpb, tp], writes=[bk[bi]])
            P.op("dve", lambda e, tg=tg, bi=bi: e.scalar_tensor_tensor(
                out=poolT[:, g, tg * 512:(tg + 1) * 512], in0=bank(bi), scalar=bps[:, g:g + 1],
                in1=poolT[:, g, tg * 512:(tg + 1) * 512], op0=ALU.add, op1=ALU.mult),
                reads=[bk[bi], t_small, t_poolT[g][tg]], writes=[t_poolT[g][tg]])

    P.op("pool", lambda e: e.memset(u_pad[:, 0:16], 0.0), writes=[t_u])
    P.op("pool", lambda e: e.memset(sA[:, 0:16], 0.0), writes=[t_sA])
    P.op("pool", lambda e: e.memset(sB[:, 0:16], 0.0), writes=[t_sB])

    for idx, cg in enumerate(CG_ORDER):
        slot = idx % 3
        if cg in (4, 5):
            pass
        if cg == 4:
            for g in range(4):
                for tg in range(4):
                    bi = inproj_fm(0, g, tg)
                    if tg % 2 == 0:
                        P.op("act", lambda e, bi=bi, tg=tg: e.activation(
                            out=u_pad[:, 16 + tg * 512:16 + (tg + 1) * 512], in_=bank(bi), func=AF.Copy),
                            reads=[bk[bi]], writes=[t_u])
                    else:
                        P.op("dve", lambda e, bi=bi, tg=tg: e.tensor_copy(
                            out=u_pad[:, 16 + tg * 512:16 + (tg + 1) * 512], in_=bank(bi)),
                            reads=[bk[bi]], writes=[t_u])
                pool_pre(g)
                for tg in range(4):
                    bi = inproj_fm(1, g, tg)
                    silu_evac(bi, poolT[:, g, tg * 512:(tg + 1) * 512], t_poolT[g][tg])
                pool_mm(g)
            load_cg(3)
            load_cg(4)
            continue
        if cg == 5:
            continue
        if cg in (0, 1):
            dstT, tT, gcol = (qT, t_qT, qg) if cg == 0 else (kT, t_kT, kg)
            pend = None
            for c in range(4):
                for tg in range(4):
                    bi = inproj_fm(slot, c, tg)
                    u = qk_stage1(bi)
                    if pend is not None:
                        qk_stage2(*pend)
                    pend = (bi, u, dstT[:, c, tg * 512:(tg + 1) * 512], tT[c][tg], gcol)
            qk_stage2(*pend)
        elif cg == 2:
            for t in range(NT):
                bi = next_psc()

                def f(e, t=t, bi=bi):
                    ins = None
                    for kc in range(KC):
                        ins = e.matmul(bank(bi), hnT[:, kc, t * 128:(t + 1) * 128], win_ring[slot][:, kc, :],
                                       start=(kc == 0), stop=(kc == KC - 1))
                    return ins
                P.op("pe", f, reads=[t_win[slot], t_hnT[t]], writes=[bk[bi]])
                if t % 2 == 0:
                    P.op("act", lambda e, t=t, bi=bi: e.activation(out=v_sb[:, t, :], in_=bank(bi), func=AF.Copy),
                         reads=[bk[bi]], writes=[t_v[t]])
                else:
                    P.op("dve", lambda e, t=t, bi=bi: e.tensor_copy(out=v_sb[:, t, :], in_=bank(bi)),
                         reads=[bk[bi]], writes=[t_v[t]])
        elif cg == 3:
            for c in range(4):
                for tg in range(4):
                    bi = inproj_fm(slot, c, tg)
                    silu_evac(bi, sgaT[:, c, tg * 512:(tg + 1) * 512], t_sga[c][tg])
        if idx == 2:
            load_cg(5)

    e_sb = [v32(RB, i * 4 * KB, [128, 2, 512]) for i in range(2)]
    sp_sb = [v16(RB, 8 * KB + i * 2 * KB, [128, 2, 512]) for i in range(3)]
    w_sb = [v16(RB, 14 * KB + i * 2 * KB, [128, 2, 512]) for i in range(2)]
    spsum = v16(RB, 18 * KB, [128, 2, 512])
    t_e, t_sp, t_w, t_spsum = toks("e", 2), toks("sp", 3), toks("w", 2), Tok("spsum")
    alias(t_e + t_sp + t_w + [t_spsum], t_win)
    t_attnT = [toks(f"attnT{c}_", 4) for c in range(4)]
    t_wout = Tok("wout")
    alias([t for l in t_attnT for t in l] + [t_wout], t_hnT)
    wsrc = wout_d.rearrange("(kc p) n -> p kc n", p=128)
    for half in range(2):
        P.dma("pool", wout_bf[:, half * 4:(half + 1) * 4, :], wsrc[:, half * 4:(half + 1) * 4, :],
              writes=[t_wout])
    for kc in range(KC):
        P.op("pool", lambda e, kc=kc: e.tensor_tensor(out=wout_bf[:, kc, :], in0=wout_bf[:, kc, :],
                                                      in1=gate_bc[:], op=ALU.mult),
             reads=[t_wout, t_bc2], writes=[t_wout])


    units = []
    for qr in range(4):
        for c in range(4):
            for j in range(4 * qr + 3, -1, -1):
                units.append((qr, c, j))
    NU = len(units)
    grp_idx = {}
    for (qr, c, j) in units:
        grp_idx.setdefault((qr, c), len(grp_idx))

    def uinfo(n):
        qr, c, j = units[n]
        s = n % 3
        q0 = max(128 * j, 512 * qr)
        wd = 512 * (qr + 1) - q0
        g = grp_idx[(qr, c)]
        return dict(qr=qr, c=c, j=j, s=s, q0=q0, wd=wd, diag=(128 * j >= 512 * qr), g=g,
                    zs=PS[s].rearrange("p (h w) -> p h w", h=2), zt=[bk[2 * s], bk[2 * s + 1]],
                    pa=bank(6), pat=bk[6], o=q0 - 512 * qr)

    def grp_begin(qr, c):
        g = grp_idx[(qr, c)]
        P.op("pe", lambda e: e.matmul(bank(6), zeros_bf[:, 0:128], zeros_bf[:, 0:512],
                                      start=True, stop=False), reads=[t_const], writes=[bk[6]])

    def grp_end(qr, c):
        g = grp_idx[(qr, c)]
        P.op("dve", lambda e: e.tensor_tensor(out=attnT[:, c, qr * 512:(qr + 1) * 512], in0=bank(6),
                                              in1=sgaT[:, c, qr * 512:(qr + 1) * 512], op=ALU.mult),
             reads=[bk[6], t_sga[c][qr]], writes=[t_attnT[c][qr]])
        if c == 3:
            outproj_enqueue(qr)

    def emit_z(n):
        u = uinfo(n)
        c, j, q0, wd, zs = u["c"], u["j"], u["q0"], u["wd"], u["zs"]

        def fz(e):
            ins = None
            for hh in range(2):
                hp = 64 * hh
                ins = e.matmul(zs[:, hh, 0:wd], kT[hp:hp + 64, c, 128 * j:128 * j + 128],
                               qT[hp:hp + 64, c, q0:q0 + wd], start=True, stop=(not u["diag"]))
            if u["diag"]:
                for hh in range(2):
                    ins = e.matmul(zs[:, hh, 0:128], ident_bf[:], maskb_bf[:], start=False, stop=True)
            return ins
        P.op("pe", fz, reads=[t_kT[c][j // 4], t_const, t_qT[c][u["qr"]]], writes=u["zt"])

    def emit_e(n):
        u = uinfo(n)
        s, wd, zs = u["s"], u["wd"], u["zs"]
        P.op("act", lambda e: e.activation(out=e_sb[n % 2][:, :, 0:wd], in_=zs[:, :, 0:wd], func=AF.Exp),
             reads=u["zt"], writes=[t_e[n % 2]])

    def emit_sp(n):
        u = uinfo(n)
        s, wd = u["s"], u["wd"]
        P.op("act", lambda e: e.activation(out=sp_sb[s][:, :, 0:wd], in_=e_sb[n % 2][:, :, 0:wd],
                                           func=AF.Ln, bias=1.0),
             reads=[t_e[n % 2]], writes=[t_sp[s]])

    def emit_p(n):
        u = uinfo(n)
        s, zs, wd, o, diag = u["s"], u["zs"], u["wd"], u["o"], u["diag"]

        def fp(e):
            ins = None
            mms = []
            for hh in range(2):
                mms.append((zs[:, hh, 0:wd], negT_bf[:], sp_sb[s][:, hh, 0:wd]))
            a2 = 128 if diag else 0
            if a2 < wd:
                for hh in range(2):
                    mms.append((zs[:, hh, a2:wd], negO_bf[:], spsum[:, hh, o + a2:o + wd]))
            for i, (o_, l_, r_) in enumerate(mms):
                ins = e.matmul(o_, l_, r_, start=False, stop=True, skip_group_check=True)
            return ins
        P.op("pe", fp, reads=[t_sp[s], t_spsum, t_const], writes=u["zt"])
        if u["j"] > 0:
            P.op("dve", lambda e: e.tensor_tensor(out=spsum[:, :, o:o + wd], in0=spsum[:, :, o:o + wd],
                                                  in1=sp_sb[s][:, :, 0:wd], op=ALU.add),
                 reads=[t_sp[s], t_spsum], writes=[t_spsum])

    def emit_w(n):
        u = uinfo(n)
        s, wd, zs = u["s"], u["wd"], u["zs"]
        P.op("act", lambda e: e.activation(out=w_sb[n % 2][:, :, 0:wd], in_=zs[:, :, 0:wd], func=AF.Exp),
             reads=u["zt"], writes=[t_w[n % 2]])

    def emit_av(n):
        u = uinfo(n)
        wd, o, c, j, pa = u["wd"], u["o"], u["c"], u["j"], u["pa"]

        def fav(e):
            ins = None
            for hh in range(2):
                h = 2 * c + hh
                ins = e.matmul(pa[64 * hh:64 * hh + 64, o:o + wd], v_sb[:, j, h * 64:(h + 1) * 64],
                               w_sb[n % 2][:, hh, 0:wd], start=False, stop=(j == 0),
                               tile_position=((0, 64) if hh == 1 else None))
            return ins
        P.op("pe", fav, reads=[t_w[n % 2], t_v[j]], writes=[u["pat"]])

    xs2 = [v32(RP, i * 4 * KB, [128, D]) for i in range(2)]
    ho = [v32(RP, 8 * KB + i * 4 * KB, [128, D]) for i in range(2)]
    t_xs2, t_ho = toks("xs2_", 2), toks("ho", 2)
    alias(t_xs2 + t_ho, [t_u, t_sA, t_sB] + t_pooled)
    op_queue = []
    op_state = {"n": 0}

    def outproj_enqueue(qr):
        for t in range(4 * qr, 4 * qr + 4):
            for nh in range(2):
                op_queue.append((t, nh))

    def outproj_step(banks):
        if not op_queue:
            return False
        t, nh = op_queue.pop(0)
        b = t % 2
        bi = banks[op_state["n"] % len(banks)]
        op_state["n"] += 1
        if nh == 0:
            P.dma("sp", xs2[b], x_d[t * 128:(t + 1) * 128, :], writes=[t_xs2[b]])

        def f(e):
            ins = None
            for kc in range(KC):
                lhs = attnT[:, kc, t * 128:(t + 1) * 128] if kc < 4 else poolT[:, kc - 4, t * 128:(t + 1) * 128]
                ins = e.matmul(bank(bi), lhs, wout_bf[:, kc, nh * 512:(nh + 1) * 512],
                               start=(kc == 0), stop=(kc == KC - 1))
            return ins
        P.op("pe", f, reads=[t_attnT[cc][t // 4] for cc in range(4)] + [t_poolT[g][t // 4] for g in range(4)]
             + [t_wout], writes=[bk[bi]])
        P.op("dve", lambda e: e.tensor_tensor(
            out=ho[b][:, nh * 512:(nh + 1) * 512], in0=bank(bi), in1=xs2[b][:, nh * 512:(nh + 1) * 512],
            op=ALU.add), reads=[bk[bi], t_xs2[b]], writes=[t_ho[b]])
        if nh == 1:
            P.dma("sp", y_d[t * 128:(t + 1) * 128, :], ho[b], reads=[t_ho[b]])
        return True

    emit_z(0)
    begun = set()
    for m in range(NU + 2):
        if m - 2 >= 0:
            k = m - 2
            gk = units[k][0:2]
            if gk not in begun:
                grp_begin(*gk)
                begun.add(gk)
            emit_w(k)
        if m + 1 < NU:
            emit_z(m + 1)
        if m - 2 >= 0:
            emit_av(k)
            if k == NU - 1 or units[k + 1][0:2] != gk:
                grp_end(*gk)
        if m < NU:
            emit_e(m)
            emit_sp(m)
            if m == 0 or units[m - 1][0:2] != units[m][0:2]:
                P.op("dve", lambda e: e.memset(spsum, 0.0), writes=[t_spsum])
            emit_p(m)
        if m % 3 == 2:
            outproj_step([7])
    while outproj_step([0, 1, 2, 3, 4, 5, 7]):
        pass

    if debug:
        for nm, ap, tk in [("d_qT", qT, [t for l in t_qT for t in l]), ("d_kT", kT, [t for l in t_kT for t in l]),
                           ("d_sgaT", sgaT, [t for l in t_sga for t in l]),
                           ("d_poolT", poolT, [t for l in t_poolT for t in l]), ("d_attnT", attnT, [t for l in t_attnT for t in l])]:
            dd = nc.dram_tensor(nm, [128, 4, S], F32, kind="ExternalOutput").ap()
            for cc in range(4):
                P.dma("pool", dd[:, cc, :], ap[:, cc, :], reads=tk)
        dd = nc.dram_tensor("d_v", [128, NT, 512], F32, kind="ExternalOutput").ap()
        for cc in range(4):
            P.dma("pool", dd[:, cc * 4:(cc + 1) * 4, :], v_sb[:, cc * 4:(cc + 1) * 4, :], reads=t_v)
    P.finish("sp")
    es.close()
    return nc


_NC_CACHE = {}


def _prep_inputs(x, c, w_ada, b_ada, norm_g, w_in, q_norm_g, k_norm_g, w_pool, b_pool, pool_scale, w_out):
    f = lambda a: np.ascontiguousarray(np.asarray(a, dtype=np.float32))
    shared = {
        "w_ada": f(w_ada[0]),
        "b_ada": f(b_ada[0]).reshape(1, 3 * D),
        "norm_g": f(norm_g[0]).reshape(1, D),
        "w_in": f(w_in[0]),
        "qg": f(np.tile(np.asarray(q_norm_g[0]), 2).reshape(128, 1)),
        "kg": f(np.tile(np.asarray(k_norm_g[0]), 2).reshape(128, 1)),
        "w_pool": f(np.asarray(w_pool[0]).transpose(1, 0, 2)),
        "bpT": f(np.asarray(b_pool[0]).T),
        "psT": f(np.asarray(pool_scale[0]).reshape(4, 128).T),
        "ps_bc": f(np.broadcast_to(np.asarray(pool_scale[0]).reshape(1, 4, 128), (128, 4, 128))),
        "w_out": f(w_out[0]),
    }
    in_maps = []
    for b in range(8):
        m = dict(shared)
        m["x"] = f(x[b])
        m["c_pk"] = f(np.asarray(c[b]).reshape(8, 128).T)
        in_maps.append(m)
    return in_maps


def kernel(x, c, w_ada, b_ada, norm_g, w_in, q_norm_g, k_norm_g, w_pool, b_pool, pool_scale, w_out):
    in_maps = _prep_inputs(x, c, w_ada, b_ada, norm_g, w_in, q_norm_g, k_norm_g,
                           w_pool, b_pool, pool_scale, w_out)
    nc = build_nc()
    res = run_bass_kernel_spmd(nc, in_maps, core_ids=list(range(8)))
    out = np.stack([np.asarray(r["y"], dtype=np.float32) for r in res.results], axis=0)
    return out
```

```python
import os
import numpy as np
from contextlib import ExitStack
import concourse.bass as bass
import concourse.mybir as mybir
from concourse.bass_utils import run_bass_kernel_spmd

F32 = mybir.dt.float32
BF16 = mybir.dt.bfloat16
AF = mybir.ActivationFunctionType
ALU = mybir.AluOpType

S = 2048
D = 1024
NT = 16
KC = 8
EPS = 1e-6
NDS = 56


class Tok:
    __slots__ = ("name", "w", "r")

    def __init__(self, name):
        self.name = name
        self.w = {}
        self.r = {}


def toks(prefix, n):
    return [Tok(f"{prefix}{i}") for i in range(n)]


def alias(new, old):
    w, r = {}, {}
    for t in old:
        for k, v in t.w.items():
            w[k] = max(w.get(k, 0), v)
        for k, v in t.r.items():
            r[k] = max(r.get(k, 0), v)
    for t in new:
        t.w = dict(w)
        t.r = dict(r)


class Prog:
    def __init__(self, nc, es):
        self.nc = nc
        self.eng = {"pe": nc.tensor, "act": nc.scalar, "dve": nc.vector,
                    "pool": nc.gpsimd, "sp": nc.sync}
        self.sem = {k: es.enter_context(nc.semaphore("s_" + k))
                    for k in ("pe", "act", "dve", "pool")}
        self.cnt = {k: 0 for k in self.sem}
        self.seen = {k: {} for k in self.eng}
        self.dsems = [es.enter_context(nc.semaphore(f"dq{i}")) for i in range(NDS)]
        self.dval = [0] * NDS
        self.dpool = {"pool": list(range(0, 24)), "sp": list(range(24, NDS)), "act": list(range(24, NDS))}
        self.dnext = {"pool": 0, "sp": 0, "act": 0}
        self.nwaits = 0

    def _semh(self, key):
        return self.sem[key] if isinstance(key, str) else self.dsems[key[1]]

    def _wait(self, eng, deps):
        for key, val in deps.items():
            if self.seen[eng].get(key, 0) >= val:
                continue
            self.eng[eng].wait_ge(self._semh(key), val)
            self.seen[eng][key] = val
            self.nwaits += 1

    def _deps(self, eng, reads, writes, is_dma=False):
        deps = {}

        def need(evs, kind):
            for key, val in evs.items():
                if not is_dma and key == eng and eng == "pe":
                    continue
                deps[key] = max(deps.get(key, 0), val)

        for t in reads:
            need(t.w, "raw")
        for t in writes:
            need(t.w, "waw")
            need(t.r, "war")
        return deps

    def op(self, eng, fn, reads=(), writes=()):
        deps = self._deps(eng, reads, writes)
        self._wait(eng, deps)
        ins = fn(self.eng[eng])
        self.cnt[eng] += 1
        ins.then_inc(self.sem[eng], 1)
        c = self.cnt[eng]
        for t in reads:
            t.r[eng] = c
        for t in writes:
            t.w = {eng: c}
            t.r = {}

    def dma(self, q, out, in_, reads=(), writes=(), **kw):
        deps = self._deps(q, reads, writes, is_dma=True)
        pool_ = self.dpool[q]
        i = pool_[self.dnext[q] % len(pool_)]
        self.dnext[q] += 1
        if self.dval[i] > 0:
            key = ("d", i)
            deps[key] = max(deps.get(key, 0), self.dval[i])
        self._wait(q, deps)
        self.dval[i] += 16
        self.eng[q].dma_start(out=out, in_=in_, **kw).then_inc(self.dsems[i], 16)
        key = ("d", i)
        for t in reads:
            t.r[key] = self.dval[i]
        for t in writes:
            t.w = {key: self.dval[i]}
            t.r = {}

    def finish(self, q="sp"):
        deps = {("d", i): v for i, v in enumerate(self.dval) if v > 0}
        for k in self.sem:
            if self.cnt[k] > 0:
                deps[k] = self.cnt[k]
        self._wait(q, deps)


def build_nc(debug=False):
    nc = bass.Bass("TRN2", target_bir_lowering=False)
    es = ExitStack()

    def dram_in(name, shape):
        return nc.dram_tensor(name, list(shape), F32, kind="ExternalInput").ap()

    x_d = dram_in("x", [S, D])
    c_d = dram_in("c_pk", [128, 8])
    wada_d = dram_in("w_ada", [D, 3 * D])
    bada_d = dram_in("b_ada", [1, 3 * D])
    ng_d = dram_in("norm_g", [1, D])
    win_d = dram_in("w_in", [D, 3 * D])
    qg_d = dram_in("qg", [128, 1])
    kg_d = dram_in("kg", [128, 1])
    wpool_d = dram_in("w_pool", [128, 4, 128])
    bp_d = dram_in("bpT", [128, 4])
    psT_d = dram_in("psT", [128, 4])
    psbc_d = dram_in("ps_bc", [128, 4, 128])
    wout_d = dram_in("w_out", [D, D])
    y_d = nc.dram_tensor("y", [S, D], F32, kind="ExternalOutput").ap()
    dbg = {}

    def dbg_out(name, shape):
        dbg[name] = nc.dram_tensor(name, list(shape), F32, kind="ExternalOutput").ap()
        return dbg[name]

    P = Prog(nc, es)

    def sb(name, shape, dt=F32):
        return es.enter_context(nc.sbuf_tensor(name, list(shape), dt))

    RA = sb("RA", [128, 24576], BF16)
    RB = sb("RB", [128, 12288], BF16)
    RC = sb("RC", [128, 16384], BF16)
    RG = sb("RG", [128, 8192], BF16)
    RQ = sb("RQ", [128, 8192], BF16)
    RP = sb("RP", [128, 17408], BF16)
    RT = sb("RT", [128, 5120], BF16)
    gate_bc = sb("gate_bc", [128, 1024], F32)

    def v16(reg, off, shape):
        n = int(np.prod(shape[1:]))
        ap = reg[:, off // 2: off // 2 + n]
        if len(shape) == 3:
            ap = ap.rearrange("p (a b) -> p a b", a=shape[1])
        return ap

    def v32(reg, off, shape):
        n = int(np.prod(shape[1:]))
        r32 = reg.bitcast(F32)
        ap = r32[:, off // 4: off // 4 + n]
        if len(shape) == 3:
            ap = ap.rearrange("p (a b) -> p a b", a=shape[1])
        return ap

    KB = 1024
    wada = v16(RA, 0, [128, 8, 3072])
    qT = v16(RA, 0, [128, 4, S])
    kT = v16(RA, 16 * KB, [128, 4, S])
    v_sb = v16(RA, 32 * KB, [128, NT, 512])
    win_ring = [v16(RB, i * 8 * KB, [128, 8, 512]) for i in range(3)]
    hnT = v16(RC, 0, [128, 8, S])
    attnT = v16(RC, 0, [128, 4, S])
    wout_bf = v16(RC, 16 * KB, [128, 8, D])
    shift_bc = v32(RG, 0, [128, D])
    gp_bc = v32(RG, 4 * KB, [128, D])
    sgaT = v16(RG, 0, [128, 4, S])
    poolT = v16(RQ, 0, [128, 4, S])

    PS = [es.enter_context(nc.psum_tensor(f"PS{i}", [128, 1024], F32)) for i in range(4)]
    PZ = [PS[2], PS[3]]

    def bank(i):
        return PS[i // 2][:, (i % 2) * 512:(i % 2 + 1) * 512]

    bk = toks("bank", 8)

    iota_f = sb("iota_f", [128, 128], F32)
    ident_bf = sb("ident_bf", [128, 128], BF16)
    negT_bf = sb("negT_bf", [128, 128], BF16)
    negO_bf = sb("negO_bf", [128, 128], BF16)
    bd_bf = sb("bd_bf", [128, 128], BF16)
    maskb_bf = sb("maskb_bf", [128, 128], BF16)
    ones_f = sb("ones_f", [1, 128], F32)
    small = sb("small", [128, 64], F32)
    cpk = small[:, 0:8]
    ce = small[:, 8:16]
    ssq = small[:, 16:32]
    rstd = small[:, 32:48]
    qg = small[:, 48:49]
    kg = small[:, 49:50]
    bpT = small[:, 50:54]
    psT = small[:, 54:58]
    bps = small[:, 58:62]
    cact_bf = sb("cact_bf", [128, 8], BF16)
    wpool_f = sb("wpool_f", [128, 4, 128], F32)
    psbc = sb("psbc", [128, 4, 128], F32)
    wpool_bf = sb("wpool_bf", [128, 4, 128], BF16)

    t_iota, t_const, t_small, t_cact = Tok("iota"), Tok("const"), Tok("small"), Tok("cact")
    t_wpf, t_psbc, t_wpb = Tok("wpf"), Tok("psbc"), Tok("wpb")

    P.op("pool", lambda e: e.iota(iota_f[:], pattern=[[1, 128]], base=0, channel_multiplier=-1,
                                  allow_small_or_imprecise_dtypes=True), writes=[t_iota])
    P.op("pool", lambda e: e.memset(negO_bf[:], -1.0), writes=[t_const])
    P.op("pool", lambda e: e.memset(bd_bf[:], 0.0), writes=[t_const])
    P.op("pool", lambda e: e.memset(bd_bf[0:64, 0:64], 1.0), writes=[t_const])
    P.op("pool", lambda e: e.memset(bd_bf[64:128, 64:128], 1.0), writes=[t_const])
    P.op("pool", lambda e: e.memset(ones_f[:], 1.0), writes=[t_const])
    zeros_bf = sb("zeros_bf", [128, 512], BF16)
    P.op("pool", lambda e: e.memset(zeros_bf[:], 0.0), writes=[t_const])
    P.op("pool", lambda e: e.memset(small[:, 16:48], 0.0), writes=[t_small])
    P.op("dve", lambda e: e.tensor_single_scalar(out=ident_bf[:], in_=iota_f[:], scalar=0.0,
                                                 op=ALU.is_equal), reads=[t_iota], writes=[t_const])
    P.op("dve", lambda e: e.tensor_scalar(out=negT_bf[:], in0=iota_f[:], scalar1=0.0, scalar2=-1.0,
                                          op0=ALU.is_le, op1=ALU.mult), reads=[t_iota], writes=[t_const])
    P.op("dve", lambda e: e.tensor_scalar(out=maskb_bf[:], in0=iota_f[:], scalar1=0.0,
                                          scalar2=-30000.0, op0=ALU.is_le, op1=ALU.mult),
         reads=[t_iota], writes=[t_const])

    RGf = RG.bitcast(F32)
    bada_row = RGf[0:1, 0:3072]
    mod_row = bada_row
    g_row = RGf[0:1, 3072:4096]
    gp_row = RT.bitcast(F32)[0:1, 0:1024]
    t_rows = Tok("rows")
    P.dma("sp", cpk, c_d, writes=[t_small])
    P.dma("sp", bada_row, bada_d, writes=[t_rows])
    P.dma("sp", g_row, ng_d, writes=[t_rows])
    P.dma("sp", qg, qg_d, writes=[t_small])
    P.dma("sp", kg, kg_d, writes=[t_small])
    P.dma("sp", bpT, bp_d, writes=[t_small])
    P.dma("sp", psT, psT_d, writes=[t_small])
    P.dma("sp", wpool_f[:], wpool_d, writes=[t_wpf])
    P.dma("sp", psbc[:], psbc_d, writes=[t_psbc])

    t_wada = toks("wada", 8)
    t_pace = toks("pace", NT)

    P.op("act", lambda e: e.activation(out=ce, in_=cpk, func=AF.Exp, scale=-1.0),
         reads=[t_small], writes=[t_small])
    P.op("dve", lambda e: e.tensor_scalar(out=ce, in0=ce, scalar1=1.0, scalar2=None, op0=ALU.add),
         reads=[t_small], writes=[t_small])
    P.op("dve", lambda e: e.reciprocal(out=ce, in_=ce), reads=[t_small], writes=[t_small])
    P.op("dve", lambda e: e.tensor_tensor(out=cact_bf[:], in0=cpk, in1=ce, op=ALU.mult),
         reads=[t_small], writes=[t_cact])
    P.op("dve", lambda e: e.tensor_scalar(out=qg, in0=qg, scalar1=0.125, scalar2=None, op0=ALU.mult),
         reads=[t_small], writes=[t_small])
    P.op("dve", lambda e: e.tensor_tensor(out=bps, in0=bpT, in1=psT, op=ALU.mult),
         reads=[t_small], writes=[t_small])
    P.op("dve", lambda e: e.tensor_tensor(out=wpool_bf[:], in0=wpool_f[:], in1=psbc[:], op=ALU.mult),
         reads=[t_wpf, t_psbc], writes=[t_wpb])

    xs = [v32(RP, i * 4 * KB, [128, D]) for i in range(3)] + [v32(RP, 30 * KB, [128, D])]
    xn_buf = v16(RP, 12 * KB, [128, 8, D])
    junk = v16(RP, 28 * KB, [128, D])
    t_xs, t_xn, t_junk = toks("xs", 4), toks("xn", 8), Tok("junk")
    t_hnT = toks("hnT", NT)
    lnv1 = small[:, 62:63]

    def psT_view(gq, kc):
        b = 4 * (gq % 2) + kc // 2
        return PS[b // 2].bitcast(BF16)[:, (b % 2) * 1024 + (kc % 2) * 512:(b % 2) * 1024 + (kc % 2) * 512 + 512], b

    for t in range(NT):
        b3, b8 = t % 4, t % 8
        gq, tl = t // 4, t % 4
        P.dma("sp", xs[b3], x_d[t * 128:(t + 1) * 128, :], writes=[t_xs[b3]])
        P.op("dve", lambda e, t=t, b3=b3: e.scalar_tensor_tensor(
            out=junk, in0=xs[b3], scalar=1.0, in1=xs[b3], op0=ALU.mult, op1=ALU.mult,
            accum_out=ssq[:, t:t + 1]), reads=[t_xs[b3], t_small], writes=[t_junk, t_small])
        P.op("act", lambda e, t=t: e.activation(out=lnv1, in_=ssq[:, t:t + 1], func=AF.Ln,
                                                scale=1.0 / D, bias=EPS),
             reads=[t_small], writes=[t_small])
        P.op("act", lambda e, t=t: e.activation(out=rstd[:, t:t + 1], in_=lnv1, func=AF.Exp, scale=-0.5),
             reads=[t_small], writes=[t_small])
        P.op("act", lambda e, t=t, b3=b3, b8=b8: e.activation(out=xn_buf[:, b8, :], in_=xs[b3], func=AF.Copy,
                                                              scale=rstd[:, t:t + 1]),
             reads=[t_xs[b3], t_small], writes=[t_xn[b8], t_pace[t]])
        if t % 2 == 1:
            kcw = t // 2
            P.dma("pool", wada[:, kcw, :], wada_d[kcw * 128:(kcw + 1) * 128, :], reads=[t_pace[t]],
                  writes=[t_wada[kcw]])

        def ftr(e, b8=b8, gq=gq, tl=tl):
            ins = None
            for kc in range(KC):
                pv, _ = psT_view(gq, kc)
                ins = e.transpose(pv[:, tl * 128:(tl + 1) * 128], xn_buf[:, b8, kc * 128:(kc + 1) * 128],
                                  ident_bf[:])
            return ins
        bks = [bk[4 * (gq % 2) + i] for i in range(4)]
        P.op("pe", ftr, reads=[t_xn[b8], t_const], writes=bks)
        if (tl == 0 and gq >= 1) or t == NT - 1:
          for gq in ([gq - 1] if t < NT - 1 else ([gq - 1, gq] if tl == 0 else [gq])):
              for kc in range(KC):
                  pv, b = psT_view(gq, kc)
                  dst = hnT[:, kc, gq * 512:(gq + 1) * 512]
                  if kc % 2 == 0:
                      P.op("act", lambda e, pv=pv, dst=dst: e.activation(out=dst, in_=pv, func=AF.Copy),
                           reads=[bk[b]], writes=t_hnT[gq * 4:(gq + 1) * 4])
                  else:
                      P.op("dve", lambda e, pv=pv, dst=dst: e.tensor_copy(out=dst, in_=pv),
                           reads=[bk[b]], writes=t_hnT[gq * 4:(gq + 1) * 4])


    for kc in range(KC):
        def f(e, kc=kc):
            ins = None
            for n in range(6):
                ins = e.matmul(bank(n)[0:1, :], cact_bf[:, kc:kc + 1], wada[:, kc, n * 512:(n + 1) * 512],
                               start=(kc == 0), stop=(kc == KC - 1))
            return ins
        P.op("pe", f, reads=[t_cact, t_wada[kc]], writes=bk[0:6])
    for n in range(6):
        P.op("dve", lambda e, n=n: e.tensor_tensor(out=mod_row[:, n * 512:(n + 1) * 512],
                                                   in0=bank(n)[0:1, :],
                                                   in1=bada_row[:, n * 512:(n + 1) * 512], op=ALU.add),
             reads=[bk[n], t_rows], writes=[t_rows])
    P.op("dve", lambda e: e.scalar_tensor_tensor(out=gp_row, in0=mod_row[:, 1024:2048], scalar=1.0,
                                                 in1=g_row, op0=ALU.add, op1=ALU.mult),
         reads=[t_rows], writes=[t_rows])
    t_bc = Tok("bc")
    modT = sb("modT", [128, 16], F32)
    t_modT = Tok("modT")
    for n in range(2):
        P.op("pe", lambda e, n=n: e.matmul(bank(n), ones_f[0:1, :], mod_row[:, 2048 + n * 512:2560 + n * 512],
                                           start=True, stop=True),
             reads=[t_rows, t_const], writes=[bk[n]])
        P.op("act", lambda e, n=n: e.activation(out=gate_bc[:, n * 512:(n + 1) * 512], in_=bank(n), func=AF.Copy),
             reads=[bk[n]], writes=[t_bc])

    def fmodT(e):
        ins = None
        for i in range(8):
            ins = e.matmul(bank(2)[:, i:i + 1], mod_row[:, i * 128:(i + 1) * 128], ones_f[0:1, 0:1],
                           start=True, stop=True)
        for i in range(8):
            ins = e.matmul(bank(2)[:, 8 + i:9 + i], gp_row[:, i * 128:(i + 1) * 128], ones_f[0:1, 0:1],
                           start=True, stop=True)
        return ins
    P.op("pe", fmodT, reads=[t_rows, t_const], writes=[bk[2]])
    P.op("dve", lambda e: e.tensor_copy(out=modT[:], in_=bank(2)[:, 0:16]), reads=[bk[2]], writes=[t_modT])

    for kc in range(KC):
        for hf in range(2):
            view = hnT[:, kc, hf * 1024:(hf + 1) * 1024]
            tk = t_hnT[hf * 8:(hf + 1) * 8]
            if (kc + hf) % 2 == 0:
                P.op("act", lambda e, view=view, kc=kc: e.activation(
                    out=view, in_=view, func=AF.Identity, scale=modT[:, 8 + kc:9 + kc], bias=modT[:, kc:kc + 1]),
                    reads=tk + [t_modT], writes=tk)
            else:
                P.op("dve", lambda e, view=view, kc=kc: e.tensor_scalar(
                    out=view, in0=view, scalar1=modT[:, 8 + kc:9 + kc], scalar2=modT[:, kc:kc + 1],
                    op0=ALU.mult, op1=ALU.add), reads=tk + [t_modT], writes=tk)

    if debug:
        d_hnT = nc.dram_tensor("d_hnT", [128, 8, S], F32, kind="ExternalOutput").ap()
        for kc in range(KC):
            P.dma("pool", d_hnT[:, kc, :], hnT[:, kc, :], reads=t_hnT)
        d_bc = nc.dram_tensor("d_bc", [128, 3, D], F32, kind="ExternalOutput").ap()
        P.dma("sp", d_bc[:, 2, :], gate_bc[:], reads=[t_bc])
        d_small = nc.dram_tensor("d_small", [128, 64], F32, kind="ExternalOutput").ap()
        P.dma("sp", d_small, small[:], reads=[t_small])

    t_qT = [toks(f"qT{c}_", 4) for c in range(4)]
    t_kT = [toks(f"kT{c}_", 4) for c in range(4)]
    t_v = toks("v", NT)
    alias([t for l in t_qT for t in l] + [t for l in t_kT for t in l] + t_v, t_wada)
    t_sga = [toks(f"sga{c}_", 4) for c in range(4)]
    alias([t for l in t_sga for t in l], [t_rows])
    t_bc2 = Tok("gate_bc")
    t_bc2.w = dict(t_bc.w)
    t_poolT = [toks(f"poolT{g}_", 4) for g in range(4)]

    UW = 16 + S
    u_pad = v32(RP, 0, [128, UW])
    sA = v32(RP, 8256, [128, UW])
    sB = v32(RP, 16512, [128, UW])
    pooled2 = [v16(RP, 24768, [128, S]), v16(RP, 24768 + 4096, [128, S])]
    invc = small[:, 0:16]
    tmp16 = sb("tmp16", [128, 16], F32)
    iota16 = sb("iota16", [128, 16], F32)
    t_u, t_sA, t_sB, t_pooled = Tok("u"), Tok("sA"), Tok("sB"), toks("pooled", 2)
    alias([t_u, t_sA, t_sB] + t_pooled, t_xs + t_xn + [t_junk])
    t_iota16 = Tok("iota16")
    P.op("pool", lambda e: e.iota(iota16[:], pattern=[[1, 16]], base=1, channel_multiplier=0,
                                  allow_small_or_imprecise_dtypes=True), writes=[t_iota16])

    sq_sb = [v16(RT, i * KB, [128, 512]) for i in range(2)]
    ln_sb = [v32(RT, 2 * KB + i * 2 * KB, [128, 512]) for i in range(2)]
    se_sb = [v32(RT, 6 * KB + i * 2 * KB, [128, 512]) for i in range(2)]
    t_sq, t_ln, t_se = toks("sq", 2), toks("ln", 2), toks("se", 2)
    alias(t_sq + t_ln + t_se, [t_rows])

    t_win = toks("win", 3)
    CG_ORDER = [4, 5, 0, 1, 2, 3]

    def load_cg(idx, after=()):
        cg = CG_ORDER[idx]
        slot = idx % 3
        src = win_d.rearrange("(kc p) n -> p kc n", p=128)
        for half in range(2):
            P.dma("pool", win_ring[slot][:, half * 4:(half + 1) * 4, :],
                  src[:, half * 4:(half + 1) * 4, cg * 512:(cg + 1) * 512], reads=list(after),
                  writes=[t_win[slot]])

    load_cg(0)
    load_cg(1, after=[t_xs[(NT - 1) % 4]])
    load_cg(2, after=[t_xs[(NT - 1) % 4]])

    cstate = {"psc": 0, "psn": 0, "u": 0}

    def next_psc():
        i = cstate["psc"]
        cstate["psc"] = (i + 1) % 4
        return i

    def inproj_fm(slot, lc, tg):
        bi = next_psc()

        def f(e):
            ins = None
            for kc in range(KC):
                ins = e.matmul(bank(bi), win_ring[slot][:, kc, lc * 128:(lc + 1) * 128],
                               hnT[:, kc, tg * 512:(tg + 1) * 512], start=(kc == 0), stop=(kc == KC - 1))
            return ins
        P.op("pe", f, reads=[t_win[slot]] + t_hnT[tg * 4:(tg + 1) * 4], writes=[bk[bi]])
        return bi

    def silu_evac(bi, dst, t_dst):
        u = cstate["u"]
        cstate["u"] = 1 - u
        P.op("act", lambda e: e.activation(out=se_sb[u], in_=bank(bi), func=AF.Exp, scale=-1.0),
             reads=[bk[bi]], writes=[t_se[u]])
        P.op("act", lambda e: e.activation(out=se_sb[u], in_=se_sb[u], func=AF.Ln, bias=1.0),
             reads=[t_se[u]], writes=[t_se[u]])
        P.op("act", lambda e: e.activation(out=se_sb[u], in_=se_sb[u], func=AF.Exp, scale=-1.0),
             reads=[t_se[u]], writes=[t_se[u]])
        P.op("dve", lambda e: e.tensor_tensor(out=dst, in0=bank(bi), in1=se_sb[u], op=ALU.mult),
             reads=[bk[bi], t_se[u]], writes=[t_dst])

    def qk_stage1(bi):
        u = cstate["psn"]
        cstate["psn"] = 1 - u
        P.op("act", lambda e: e.activation(out=sq_sb[u], in_=bank(bi), func=AF.Square),
             reads=[bk[bi]], writes=[t_sq[u]])
        return u

    def qk_stage2(bi, u, dst, t_dst, gcol):
        bn = 4 + u
        P.op("pe", lambda e: e.matmul(bank(bn), bd_bf[:], sq_sb[u], start=True, stop=True),
             reads=[t_sq[u], t_const], writes=[bk[bn]])
        P.op("act", lambda e: e.activation(out=ln_sb[u], in_=bank(bn), func=AF.Ln, scale=1.0 / 64.0, bias=EPS),
             reads=[bk[bn]], writes=[t_ln[u]])
        P.op("act", lambda e: e.activation(out=ln_sb[u], in_=ln_sb[u], func=AF.Exp, scale=-0.5),
             reads=[t_ln[u]], writes=[t_ln[u]])
        P.op("dve", lambda e: e.scalar_tensor_tensor(out=dst, in0=bank(bi), scalar=gcol, in1=ln_sb[u],
                                                     op0=ALU.mult, op1=ALU.mult),
             reads=[bk[bi], t_ln[u], t_small], writes=[t_dst])

    def pool_pre(g):
        win = 2 << g
        pooled, tp = pooled2[g % 2], t_pooled[g % 2]
        sbuf = [sA, sB]
        tk = [t_sA, t_sB]
        src, tsrc = u_pad, t_u
        sh = 1
        k = 0
        while sh < win:
            dst, tdst = sbuf[k % 2], tk[k % 2]
            P.op("dve", lambda e, src=src, dst=dst, sh=sh: e.tensor_tensor(
                out=dst[:, 16:UW], in0=src[:, 16:UW], in1=src[:, 16 - sh:UW - sh], op=ALU.add),
                reads=[tsrc], writes=[tdst])
            src, tsrc = dst, tdst
            sh *= 2
            k += 1
        P.op("dve", lambda e, src=src: e.scalar_tensor_tensor(
            out=pooled, in0=src[:, 16:UW], scalar=1.0 / win, in1=u_pad[:, 16:UW],
            op0=ALU.mult, op1=ALU.subtract), reads=[tsrc, t_u], writes=[tp])
        P.op("dve", lambda e: e.tensor_scalar(out=invc, in0=iota16[:], scalar1=float(win), scalar2=None,
                                              op0=ALU.min), reads=[t_iota16, t_small], writes=[t_small])
        P.op("dve", lambda e: e.reciprocal(out=invc, in_=invc), reads=[t_small], writes=[t_small])
        P.op("dve", lambda e, src=src: e.tensor_tensor(out=tmp16[:], in0=src[:, 16:32], in1=invc, op=ALU.mult),
             reads=[tsrc, t_small], writes=[t_small])
        P.op("dve", lambda e: e.tensor_tensor(out=pooled[:, 0:16], in0=tmp16[:], in1=u_pad[:, 16:32],
                                              op=ALU.subtract), reads=[t_small, t_u, tp], writes=[tp])

    def pool_mm(g):
        pooled, tp = pooled2[g % 2], t_pooled[g % 2]
        for tg in range(4):
            bi = 6 + (tg % 2)
            P.op("pe", lambda e, tg=tg, bi=bi: e.matmul(bank(bi), wpool_bf[:, g, :],
                                                        pooled[:, tg * 512:(tg + 1) * 512],
                                                        start=True, stop=True),
                 reads=[t_wpb, tp], writes=[bk[bi]])
            P.op("dve", lambda e, tg=tg, bi=bi: e.scalar_tensor_tensor(
                out=poolT[:, g, tg * 512:(tg + 1) * 512], in0=bank(bi), scalar=bps[:, g:g + 1],
                in1=poolT[:, g, tg * 512:(tg + 1) * 512], op0=ALU.add, op1=ALU.mult),
                reads=[bk[bi], t_small, t_poolT[g][tg]], writes=[t_poolT[g][tg]])

    P.op("pool", lambda e: e.memset(u_pad[:, 0:16], 0.0), writes=[t_u])
    P.op("pool", lambda e: e.memset(sA[:, 0:16], 0.0), writes=[t_sA])
    P.op("pool", lambda e: e.memset(sB[:, 0:16], 0.0), writes=[t_sB])

    for idx, cg in enumerate(CG_ORDER):
        slot = idx % 3
        if cg in (4, 5):
            pass
        if cg == 4:
            for g in range(4):
                for tg in range(4):
                    bi = inproj_fm(0, g, tg)
                    if tg % 2 == 0:
                        P.op("act", lambda e, bi=bi, tg=tg: e.activation(
                            out=u_pad[:, 16 + tg * 512:16 + (tg + 1) * 512], in_=bank(bi), func=AF.Copy),
                            reads=[bk[bi]], writes=[t_u])
                    else:
                        P.op("dve", lambda e, bi=bi, tg=tg: e.tensor_copy(
                            out=u_pad[:, 16 + tg * 512:16 + (tg + 1) * 512], in_=bank(bi)),
                            reads=[bk[bi]], writes=[t_u])
                pool_pre(g)
                for tg in range(4):
                    bi = inproj_fm(1, g, tg)
                    silu_evac(bi, poolT[:, g, tg * 512:(tg + 1) * 512], t_poolT[g][tg])
                pool_mm(g)
            load_cg(3)
            load_cg(4)
            continue
        if cg == 5:
            continue
        if cg in (0, 1):
            dstT, tT, gcol = (qT, t_qT, qg) if cg == 0 else (kT, t_kT, kg)
            pend = None
            for c in range(4):
                for tg in range(4):
                    bi = inproj_fm(slot, c, tg)
                    u = qk_stage1(bi)
                    if pend is not None:
                        qk_stage2(*pend)
                    pend = (bi, u, dstT[:, c, tg * 512:(tg + 1) * 512], tT[c][tg], gcol)
            qk_stage2(*pend)
        elif cg == 2:
            for t in range(NT):
                bi = next_psc()

                def f(e, t=t, bi=bi):
                    ins = None
                    for kc in range(KC):
                        ins = e.matmul(bank(bi), hnT[:, kc, t * 128:(t + 1) * 128], win_ring[slot][:, kc, :],
                                       start=(kc == 0), stop=(kc == KC - 1))
                    return ins
                P.op("pe", f, reads=[t_win[slot], t_hnT[t]], writes=[bk[bi]])
                if t % 2 == 0:
                    P.op("act", lambda e, t=t, bi=bi: e.activation(out=v_sb[:, t, :], in_=bank(bi), func=AF.Copy),
                         reads=[bk[bi]], writes=[t_v[t]])
                else:
                    P.op("dve", lambda e, t=t, bi=bi: e.tensor_copy(out=v_sb[:, t, :], in_=bank(bi)),
                         reads=[bk[bi]], writes=[t_v[t]])
        elif cg == 3:
            for c in range(4):
                for tg in range(4):
                    bi = inproj_fm(slot, c, tg)
                    silu_evac(bi, sgaT[:, c, tg * 512:(tg + 1) * 512], t_sga[c][tg])
        if idx == 2:
            load_cg(5)

    e_sb = [v32(RB, i * 4 * KB, [128, 2, 512]) for i in range(2)]
    sp_sb = [v16(RB, 8 * KB + i * 2 * KB, [128, 2, 512]) for i in range(3)]
    w_sb = [v16(RB, 14 * KB + i * 2 * KB, [128, 2, 512]) for i in range(2)]
    spsum = v16(RB, 18 * KB, [128, 2, 512])
    t_e, t_sp, t_w, t_spsum = toks("e", 2), toks("sp", 3), toks("w", 2), Tok("spsum")
    alias(t_e + t_sp + t_w + [t_spsum], t_win)
    t_attnT = [toks(f"attnT{c}_", 4) for c in range(4)]
    t_wout = Tok("wout")
    alias([t for l in t_attnT for t in l] + [t_wout], t_hnT)
    wsrc = wout_d.rearrange("(kc p) n -> p kc n", p=128)
    for half in range(2):
        P.dma("pool", wout_bf[:, half * 4:(half + 1) * 4, :], wsrc[:, half * 4:(half + 1) * 4, :],
              writes=[t_wout])
    for kc in range(KC):
        P.op("pool", lambda e, kc=kc: e.tensor_tensor(out=wout_bf[:, kc, :], in0=wout_bf[:, kc, :],
                                                      in1=gate_bc[:], op=ALU.mult),
             reads=[t_wout, t_bc2], writes=[t_wout])


    units = []
    for qr in range(4):
        for c in range(4):
            for j in range(4 * qr + 3, -1, -1):
                units.append((qr, c, j))
    NU = len(units)
    grp_idx = {}
    for (qr, c, j) in units:
        grp_idx.setdefault((qr, c), len(grp_idx))

    def uinfo(n):
        qr, c, j = units[n]
        s = n % 3
        q0 = max(128 * j, 512 * qr)
        wd = 512 * (qr + 1) - q0
        g = grp_idx[(qr, c)]
        return dict(qr=qr, c=c, j=j, s=s, q0=q0, wd=wd, diag=(128 * j >= 512 * qr), g=g,
                    zs=PS[s].rearrange("p (h w) -> p h w", h=2), zt=[bk[2 * s], bk[2 * s + 1]],
                    pa=bank(6), pat=bk[6], o=q0 - 512 * qr)

    def grp_begin(qr, c):
        g = grp_idx[(qr, c)]
        P.op("pe", lambda e: e.matmul(bank(6), zeros_bf[:, 0:128], zeros_bf[:, 0:512],
                                      start=True, stop=False), reads=[t_const], writes=[bk[6]])

    def grp_end(qr, c):
        g = grp_idx[(qr, c)]
        P.op("dve", lambda e: e.tensor_tensor(out=attnT[:, c, qr * 512:(qr + 1) * 512], in0=bank(6),
                                              in1=sgaT[:, c, qr * 512:(qr + 1) * 512], op=ALU.mult),
             reads=[bk[6], t_sga[c][qr]], writes=[t_attnT[c][qr]])
        if c == 3:
            outproj_enqueue(qr)

    def emit_z(n):
        u = uinfo(n)
        c, j, q0, wd, zs = u["c"], u["j"], u["q0"], u["wd"], u["zs"]

        def fz(e):
            ins = None
            for hh in range(2):
                hp = 64 * hh
                ins = e.matmul(zs[:, hh, 0:wd], kT[hp:hp + 64, c, 128 * j:128 * j + 128],
                               qT[hp:hp + 64, c, q0:q0 + wd], start=True, stop=(not u["diag"]))
            if u["diag"]:
                for hh in range(2):
                    ins = e.matmul(zs[:, hh, 0:128], ident_bf[:], maskb_bf[:], start=False, stop=True)
            return ins
        P.op("pe", fz, reads=[t_kT[c][j // 4], t_const, t_qT[c][u["qr"]]], writes=u["zt"])

    def emit_e(n):
        u = uinfo(n)
        s, wd, zs = u["s"], u["wd"], u["zs"]
        P.op("act", lambda e: e.activation(out=e_sb[n % 2][:, :, 0:wd], in_=zs[:, :, 0:wd], func=AF.Exp),
             reads=u["zt"], writes=[t_e[n % 2]])

    def emit_sp(n):
        u = uinfo(n)
        s, wd = u["s"], u["wd"]
        P.op("act", lambda e: e.activation(out=sp_sb[s][:, :, 0:wd], in_=e_sb[n % 2][:, :, 0:wd],
                                           func=AF.Ln, bias=1.0),
             reads=[t_e[n % 2]], writes=[t_sp[s]])

    def emit_p(n):
        u = uinfo(n)
        s, zs, wd, o, diag = u["s"], u["zs"], u["wd"], u["o"], u["diag"]

        def fp(e):
            ins = None
            mms = []
            for hh in range(2):
                mms.append((zs[:, hh, 0:wd], negT_bf[:], sp_sb[s][:, hh, 0:wd]))
            a2 = 128 if diag else 0
            if a2 < wd:
                for hh in range(2):
                    mms.append((zs[:, hh, a2:wd], negO_bf[:], spsum[:, hh, o + a2:o + wd]))
            for i, (o_, l_, r_) in enumerate(mms):
                ins = e.matmul(o_, l_, r_, start=False, stop=True, skip_group_check=True)
            return ins
        P.op("pe", fp, reads=[t_sp[s], t_spsum, t_const], writes=u["zt"])
        if u["j"] > 0:
            P.op("dve", lambda e: e.tensor_tensor(out=spsum[:, :, o:o + wd], in0=spsum[:, :, o:o + wd],
                                                  in1=sp_sb[s][:, :, 0:wd], op=ALU.add),
                 reads=[t_sp[s], t_spsum], writes=[t_spsum])

    def emit_w(n):
        u = uinfo(n)
        s, wd, zs = u["s"], u["wd"], u["zs"]
        P.op("act", lambda e: e.activation(out=w_sb[n % 2][:, :, 0:wd], in_=zs[:, :, 0:wd], func=AF.Exp),
             reads=u["zt"], writes=[t_w[n % 2]])

    def emit_av(n):
        u = uinfo(n)
        wd, o, c, j, pa = u["wd"], u["o"], u["c"], u["j"], u["pa"]

        def fav(e):
            ins = None
            for hh in range(2):
                h = 2 * c + hh
                ins = e.matmul(pa[64 * hh:64 * hh + 64, o:o + wd], v_sb[:, j, h * 64:(h + 1) * 64],
                               w_sb[n % 2][:, hh, 0:wd], start=False, stop=(j == 0),
                               tile_position=((0, 64) if hh == 1 else None))
            return ins
        P.op("pe", fav, reads=[t_w[n % 2], t_v[j]], writes=[u["pat"]])

    xs2 = [v32(RP, i * 4 * KB, [128, D]) for i in range(2)]
    ho = [v32(RP, 8 * KB + i * 4 * KB, [128, D]) for i in range(2)]
    t_xs2, t_ho = toks("xs2_", 2), toks("ho", 2)
    alias(t_xs2 + t_ho, [t_u, t_sA, t_sB] + t_pooled)
    xs3 = [v32(RP, 16 * KB + i * 4 * KB, [128, D]) for i in range(4)]
    t_xs3 = toks("xs3_", 4)
    alias(t_xs3, [t_u, t_sA, t_sB] + t_pooled)
    for i_ in range(4):
        P.dma("sp", xs3[i_], x_d[(12 + i_) * 128:(13 + i_) * 128, :], writes=[t_xs3[i_]])
    op_queue = []
    op_state = {"n": 0}

    def outproj_enqueue(qr):
        for t in range(4 * qr, 4 * qr + 4):
            for nh in range(2):
                op_queue.append((t, nh))

    def outproj_step(banks):
        if not op_queue:
            return False
        t, nh = op_queue.pop(0)
        b = t % 2
        bi = banks[op_state["n"] % len(banks)]
        op_state["n"] += 1
        if t >= 12:
            xin, t_xin = xs3[t - 12], t_xs3[t - 12]
        else:
            xin, t_xin = xs2[b], t_xs2[b]
            if nh == 0:
                P.dma("sp", xs2[b], x_d[t * 128:(t + 1) * 128, :], writes=[t_xs2[b]])

        def f(e):
            ins = None
            for kc in range(KC):
                lhs = attnT[:, kc, t * 128:(t + 1) * 128] if kc < 4 else poolT[:, kc - 4, t * 128:(t + 1) * 128]
                ins = e.matmul(bank(bi), lhs, wout_bf[:, kc, nh * 512:(nh + 1) * 512],
                               start=(kc == 0), stop=(kc == KC - 1))
            return ins
        P.op("pe", f, reads=[t_attnT[cc][t // 4] for cc in range(4)] + [t_poolT[g][t // 4] for g in range(4)]
             + [t_wout], writes=[bk[bi]])
        P.op("dve", lambda e: e.tensor_tensor(
            out=ho[b][:, nh * 512:(nh + 1) * 512], in0=bank(bi), in1=xin[:, nh * 512:(nh + 1) * 512],
            op=ALU.add), reads=[bk[bi], t_xin], writes=[t_ho[b]])
        if nh == 1:
            P.dma("sp", y_d[t * 128:(t + 1) * 128, :], ho[b], reads=[t_ho[b]])
        return True

    emit_z(0)
    begun = set()
    for m in range(NU + 2):
        if m - 2 >= 0:
            k = m - 2
            gk = units[k][0:2]
            if gk not in begun:
                grp_begin(*gk)
                begun.add(gk)
            emit_w(k)
        if m + 1 < NU:
            emit_z(m + 1)
        if m - 2 >= 0:
            emit_av(k)
            if k == NU - 1 or units[k + 1][0:2] != gk:
                grp_end(*gk)
        if m < NU:
            emit_e(m)
            emit_sp(m)
            if m == 0 or units[m - 1][0:2] != units[m][0:2]:
                P.op("dve", lambda e: e.memset(spsum, 0.0), writes=[t_spsum])
            emit_p(m)
        if m % 3 == 2:
            outproj_step([7])
    while outproj_step([0, 1, 2, 3, 4, 5, 7]):
        pass

    if debug:
        for nm, ap, tk in [("d_qT", qT, [t for l in t_qT for t in l]), ("d_kT", kT, [t for l in t_kT for t in l]),
                           ("d_sgaT", sgaT, [t for l in t_sga for t in l]),
                           ("d_poolT", poolT, [t for l in t_poolT for t in l]), ("d_attnT", attnT, [t for l in t_attnT for t in l])]:
            dd = nc.dram_tensor(nm, [128, 4, S], F32, kind="ExternalOutput").ap()
            for cc in range(4):
                P.dma("pool", dd[:, cc, :], ap[:, cc, :], reads=tk)
        dd = nc.dram_tensor("d_v", [128, NT, 512], F32, kind="ExternalOutput").ap()
        for cc in range(4):
            P.dma("pool", dd[:, cc * 4:(cc + 1) * 4, :], v_sb[:, cc * 4:(cc + 1) * 4, :], reads=t_v)
    P.finish("sp")
    es.close()
    return nc


_NC_CACHE = {}


def _prep_inputs(x, c, w_ada, b_ada, norm_g, w_in, q_norm_g, k_norm_g, w_pool, b_pool, pool_scale, w_out):
    f = lambda a: np.ascontiguousarray(np.asarray(a, dtype=np.float32))
    shared = {
        "w_ada": f(w_ada[0]),
        "b_ada": f(b_ada[0]).reshape(1, 3 * D),
        "norm_g": f(norm_g[0]).reshape(1, D),
        "w_in": f(w_in[0]),
        "qg": f(np.tile(np.asarray(q_norm_g[0]), 2).reshape(128, 1)),
        "kg": f(np.tile(np.asarray(k_norm_g[0]), 2).reshape(128, 1)),
        "w_pool": f(np.asarray(w_pool[0]).transpose(1, 0, 2)),
        "bpT": f(np.asarray(b_pool[0]).T),
        "psT": f(np.asarray(pool_scale[0]).reshape(4, 128).T),
        "ps_bc": f(np.broadcast_to(np.asarray(pool_scale[0]).reshape(1, 4, 128), (128, 4, 128))),
        "w_out": f(w_out[0]),
    }
    in_maps = []
    for b in range(8):
        m = dict(shared)
        m["x"] = f(x[b])
        m["c_pk"] = f(np.asarray(c[b]).reshape(8, 128).T)
        in_maps.append(m)
    return in_maps


def kernel(x, c, w_ada, b_ada, norm_g, w_in, q_norm_g, k_norm_g, w_pool, b_pool, pool_scale, w_out):
    in_maps = _prep_inputs(x, c, w_ada, b_ada, norm_g, w_in, q_norm_g, k_norm_g,
                           w_pool, b_pool, pool_scale, w_out)
    nc = build_nc()
    res = run_bass_kernel_spmd(nc, in_maps, core_ids=list(range(8)))
    out = np.stack([np.asarray(r["y"], dtype=np.float32) for r in res.results], axis=0)
    return out
```
